# Optimizing a Trainium2 kernel written in Bass

```python
import jax, jax.numpy as jnp
from jax import lax
import numpy as np


D_MODEL = 1024
BATCH = 2
SEQ = 8192
DEPTH = 2

N_MEM = 256
D_MIX = D_MODEL
CONV_HEADS = 4
CONV_DIM = D_MIX // 4
CONV_WIDTH = 3
REC_HEADS = 4
REC_DIM = D_MIX // 2
REC_HEAD_DIM = REC_DIM // REC_HEADS
REC_EXPAND = 128
REC_FDIM = REC_HEADS * REC_EXPAND
CHUNK = 64
POOL_DIM = D_MIX - CONV_DIM - REC_DIM
POOL_WINDOWS = (2, 4, 8, 16)
POOL_GROUP = POOL_DIM // len(POOL_WINDOWS)
D_IN = 3 * CONV_DIM + 2 * REC_FDIM + 2 * REC_DIM + POOL_DIM
CA_HEADS = 4
CA_HEAD_DIM = D_MODEL // CA_HEADS
D_FF = 2816
ALPHA = (2.0 * DEPTH) ** 0.25
BETA = (8.0 * DEPTH) ** -0.25
LN_EPS = 1e-5
RMS_EPS = 1e-6

kernel_name = 'hymba_style_conv_hgrn2_pool_hybrid'


def layer_norm(x, g, b):
    xf = x.astype(jnp.float32)
    mu = jnp.mean(xf, axis=-1, keepdims=True)
    var = jnp.mean(jnp.square(xf - mu), axis=-1, keepdims=True)
    return ((xf - mu) * lax.rsqrt(var + LN_EPS)).astype(x.dtype) * g + b


def swiglu(x, w_gate, w_up, w_down):
    return (jax.nn.silu(x @ w_gate) * (x @ w_up)) @ w_down


def short_gated_conv(h, b_gate, c_gate, w):
    u = c_gate * h
    s = u.shape[1]
    up = jnp.pad(u, ((0, 0), (CONV_WIDTH - 1, 0), (0, 0)))
    y = sum(w[k] * up[:, k:k + s] for k in range(CONV_WIDTH))
    return b_gate * y


def hgrn2(q, f_logit, v, g, lb, norm_g):
    bsz, s, _ = q.shape
    n_chunks = s // CHUNK
    f32 = jnp.float32
    f = lb + (1.0 - lb) * jax.nn.sigmoid(f_logit.astype(f32))
    logf = jnp.log(f)
    k = 1.0 - f

    def to_chunks(t, d):
        return t.reshape(bsz, n_chunks, CHUNK, REC_HEADS, d).transpose(1, 0, 3, 2, 4)

    qc = to_chunks(q.astype(f32), REC_EXPAND)
    kc = to_chunks(k, REC_EXPAND)
    lc = to_chunks(logf, REC_EXPAND)
    vc = to_chunks(v.astype(f32), REC_HEAD_DIM)
    mask = jnp.tril(jnp.ones((CHUNK, CHUNK), dtype=bool))

    def step(state, inp):
        qb, kb, vb, lb_ = inp
        b = jnp.cumsum(lb_, axis=2)
        diff = b[:, :, :, None, :] - b[:, :, None, :, :]
        decay = jnp.exp(jnp.where(mask[:, :, None], diff, -jnp.inf))
        attn = jnp.einsum('bhtd,bhsd,bhtsd->bhts', qb, kb, decay)
        o = (jnp.einsum('bhts,bhsv->bhtv', attn, vb)
             + jnp.einsum('bhtd,bhdv->bhtv', qb * jnp.exp(b), state))
        b_last = b[:, :, -1:, :]
        new_state = (jnp.exp(b_last[:, :, 0, :])[..., None] * state
                     + jnp.einsum('bhsd,bhsv->bhdv', kb * jnp.exp(b_last - b), vb))
        return new_state, o

    s0 = jnp.zeros((bsz, REC_HEADS, REC_EXPAND, REC_HEAD_DIM), f32)
    _, o = lax.scan(step, s0, (qc, kc, vc, lc))
    o = o.transpose(1, 0, 3, 2, 4).reshape(bsz, s, REC_HEADS, REC_HEAD_DIM)
    o = o * lax.rsqrt(jnp.mean(jnp.square(o), axis=-1, keepdims=True) + RMS_EPS)
    o = o * norm_g.reshape(REC_HEADS, REC_HEAD_DIM).astype(f32)
    o = o.reshape(bsz, s, REC_DIM) * jax.nn.sigmoid(g.astype(f32))
    return o.astype(v.dtype)


def multiscale_pool(u, w_pool, scale):
    bsz, s, _ = u.shape
    uf = u.astype(jnp.float32)
    cs = jnp.cumsum(uf, axis=1)
    t = jnp.arange(s)
    groups = []
    for gi, w in enumerate(POOL_WINDOWS):
        sl = slice(gi * POOL_GROUP, (gi + 1) * POOL_GROUP)
        c = cs[:, :, sl]
        win_sum = c - jnp.pad(c, ((0, 0), (w, 0), (0, 0)))[:, :s]
        count = jnp.minimum(t + 1, w).astype(jnp.float32)[None, :, None]
        groups.append(win_sum / count - uf[:, :, sl])
    p = jnp.stack(groups, axis=2).astype(u.dtype)
    y = jnp.einsum('bsgc,gcd->bsgd', p, w_pool).reshape(bsz, s, POOL_DIM)
    return y * scale


def cross_attn(x, mem, wq, wk, wv, wo):
    bsz, s, _ = x.shape
    m = mem.shape[1]
    q = (x @ wq).reshape(bsz, s, CA_HEADS, CA_HEAD_DIM)
    k = (mem @ wk).reshape(bsz, m, CA_HEADS, CA_HEAD_DIM)
    v = (mem @ wv).reshape(bsz, m, CA_HEADS, CA_HEAD_DIM)
    sc = jnp.einsum('bshd,bmhd->bhsm', q, k).astype(jnp.float32) * (CA_HEAD_DIM ** -0.5)
    p = jax.nn.softmax(sc, axis=-1).astype(v.dtype)
    o = jnp.einsum('bhsm,bmhd->bshd', p, v).reshape(bsz, s, D_MODEL)
    return o @ wo


def setup_inputs(seed: int = 0) -> dict:
    key = jax.random.key(seed)
    ks = jax.random.split(key, 24)
    f32 = jnp.float32
    nrm = lambda k, shp, sc: jax.random.normal(k, shp, f32) * sc
    return {
        'x': nrm(ks[0], (BATCH, SEQ, D_MODEL), 1.0),
        'mem': nrm(ks[1], (BATCH, N_MEM, D_MODEL), 1.0),
        'ffn1_gate': nrm(ks[2], (DEPTH, D_MODEL, D_FF), BETA * D_MODEL ** -0.5),
        'ffn1_up': nrm(ks[3], (DEPTH, D_MODEL, D_FF), BETA * D_MODEL ** -0.5),
        'ffn1_down': nrm(ks[4], (DEPTH, D_FF, D_MODEL), BETA * D_FF ** -0.5),
        'w_in': nrm(ks[5], (DEPTH, D_MODEL, D_IN), D_MODEL ** -0.5),
        'conv_w': nrm(ks[6], (DEPTH, CONV_WIDTH, CONV_DIM), CONV_WIDTH ** -0.5),
        'rec_lb': nrm(ks[7], (DEPTH, REC_FDIM), 0.5),
        'rec_norm_g': 1.0 + nrm(ks[8], (DEPTH, REC_DIM), 0.02),
        'pool_w': nrm(ks[9], (DEPTH, len(POOL_WINDOWS), POOL_GROUP, POOL_GROUP), POOL_GROUP ** -0.5),
        'pool_scale': 1.0 + nrm(ks[10], (DEPTH, POOL_DIM), 0.02),
        'w_out': nrm(ks[11], (DEPTH, D_MIX, D_MODEL), BETA * D_MIX ** -0.5),
        'ca_q': nrm(ks[12], (DEPTH, D_MODEL, D_MODEL), D_MODEL ** -0.5),
        'ca_k': nrm(ks[13], (DEPTH, D_MODEL, D_MODEL), D_MODEL ** -0.5),
        'ca_v': nrm(ks[14], (DEPTH, D_MODEL, D_MODEL), BETA * D_MODEL ** -0.5),
        'ca_o': nrm(ks[15], (DEPTH, D_MODEL, D_MODEL), BETA * D_MODEL ** -0.5),
        'ffn2_gate': nrm(ks[16], (DEPTH, D_MODEL, D_FF), BETA * D_MODEL ** -0.5),
        'ffn2_up': nrm(ks[17], (DEPTH, D_MODEL, D_FF), BETA * D_MODEL ** -0.5),
        'ffn2_down': nrm(ks[18], (DEPTH, D_FF, D_MODEL), BETA * D_FF ** -0.5),
        'ln_g': 1.0 + nrm(ks[19], (DEPTH, 4, D_MODEL), 0.02),
        'ln_b': nrm(ks[20], (DEPTH, 4, D_MODEL), 0.02),
    }


def reference(x, mem, ffn1_gate, ffn1_up, ffn1_down, w_in, conv_w, rec_lb, rec_norm_g,
              pool_w, pool_scale, w_out, ca_q, ca_k, ca_v, ca_o,
              ffn2_gate, ffn2_up, ffn2_down, ln_g, ln_b):
    sm = jax.nn.softmax(rec_lb.astype(jnp.float32), axis=0)
    lbs = jnp.cumsum(sm, axis=0) - sm[0:1]
    split_at = np.cumsum([CONV_DIM, CONV_DIM, CONV_DIM, REC_FDIM, REC_FDIM, REC_DIM, REC_DIM])
    h = x
    for l in range(DEPTH):
        h = layer_norm(ALPHA * h + 0.5 * swiglu(h, ffn1_gate[l], ffn1_up[l], ffn1_down[l]),
                       ln_g[l, 0], ln_b[l, 0])
        z = h @ w_in[l]
        cb, cc, ch, rq, rf, ri, rg, pu = jnp.split(z, split_at, axis=-1)
        y_conv = short_gated_conv(ch, cb, cc, conv_w[l])
        y_rec = hgrn2(rq, rf, ri, rg, lbs[l], rec_norm_g[l])
        y_pool = multiscale_pool(pu, pool_w[l], pool_scale[l])
        mix = jnp.concatenate([y_conv, y_rec, y_pool], axis=-1) @ w_out[l]
        h = layer_norm(ALPHA * h + mix, ln_g[l, 1], ln_b[l, 1])
        h = layer_norm(ALPHA * h + cross_attn(h, mem, ca_q[l], ca_k[l], ca_v[l], ca_o[l]),
                       ln_g[l, 2], ln_b[l, 2])
        h = layer_norm(ALPHA * h + 0.5 * swiglu(h, ffn2_gate[l], ffn2_up[l], ffn2_down[l]),
                       ln_g[l, 3], ln_b[l, 3])
    return h
```

```python
import contextlib
import numpy as np
import concourse.bass as bass
import concourse.mybir as mybir
from concourse.bass_utils import run_bass_kernel_spmd

F32 = mybir.dt.float32
BF16 = mybir.dt.bfloat16
AF = mybir.ActivationFunctionType
ALU = mybir.AluOpType
AX = mybir.AxisListType

D = 1024
DFF = 2816
NFC = DFF // 128
T = 2048
NT = T // 128
SEQ = 8192
DEPTH = 2
ALPHA = (2.0 * DEPTH) ** 0.25
LN_EPS = 1e-5
RMS_EPS = 1e-6
NMEM = 256
ENGS = ("pe", "act", "dve", "pool", "sp")


class Buf:
    def __init__(self, name):
        self.name = name
        self.w = None
        self.r = {}


class Prog:
    def __init__(self, nc):
        self.nc = nc
        self.q = {e: [] for e in ENGS}
        self.cnt = {e: 0 for e in ENGS}
        self.seen = {}
        self.dsem = {}
        self.out_tokens = []

    def _deps(self, eng, reads, writes):
        deps = []
        for b in reads:
            if b.w is not None:
                deps.append(b.w)
        for b in writes:
            if b.w is not None:
                deps.append(b.w)
            for k, v in b.r.items():
                deps.append((k, v))
        for key, val in deps:
            if key == "pe" and eng == "pe":
                continue
            if self.seen.get((eng, key), 0) >= val:
                continue
            self.seen[(eng, key)] = val
            self.q[eng].append(("wait", key, val))

    def _mark(self, tok, reads, writes):
        for b in reads:
            if b.r.get(tok[0], 0) < tok[1]:
                b.r[tok[0]] = tok[1]
        for b in writes:
            b.w = tok
            b.r = {}

    def op(self, eng, fn, reads=(), writes=()):
        self._deps(eng, reads, writes)
        self.cnt[eng] += 1
        tok = (eng, self.cnt[eng])
        self.q[eng].append(("op", fn))
        self._mark(tok, reads, writes)
        return tok

    def dma(self, eng, fn, reads=(), writes=(), is_out=False):
        self._deps(eng, reads, writes)
        owner = writes[0] if (writes and not is_out) else reads[0]
        key = "d_" + owner.name + ("_o" if is_out else "_i")
        self.dsem[key] = self.dsem.get(key, 0) + 16
        tok = (key, self.dsem[key])
        self.q[eng].append(("dma", fn, key))
        self._mark(tok, reads, writes)
        if is_out:
            self.out_tokens.append(tok)
        return tok

    def finish(self):
        last = {}
        for k, v in self.out_tokens:
            last[k] = max(last.get(k, 0), v)
        for k, v in last.items():
            self.q["sp"].append(("wait", k, v))

    def emit(self):
        nc = self.nc
        with contextlib.ExitStack() as es:
            sems = {}
            for e in ENGS[:4]:
                sems[e] = es.enter_context(nc.semaphore("s_" + e))
            for k in self.dsem:
                sems[k] = es.enter_context(nc.semaphore(k))
            block = es.enter_context(nc.Block())
            q = self.q

            def replay(name, e):
                for it in q[name]:
                    if it[0] == "wait":
                        e.wait_ge(sems[it[1]], it[2])
                    elif it[0] == "op":
                        it[1](e).then_inc(sems[name], 1)
                    else:
                        it[1](e).then_inc(sems[it[2]], 16)

            @block.tensor
            def _(e):
                replay("pe", e)

            @block.scalar
            def _(e):
                replay("act", e)

            @block.vector
            def _(e):
                replay("dve", e)

            @block.gpsimd
            def _(e):
                replay("pool", e)

            @block.sync
            def _(e):
                replay("sp", e)


class PBuild:
    def __init__(self, part_a, part_b):
        self.part_a, self.part_b = part_a, part_b
        nc = self.nc = bass.Bass("TRN2", target_bir_lowering=False)
        self.P = Prog(nc)
        self.es = contextlib.ExitStack()
        dt = nc.dram_tensor
        self.d = {}
        ins = {"h_in": [T, D], "ident": [128, 128]}
        if part_a:
            ins.update({"mixT": [D, T], "mem": [NMEM, D], "w_out": [2, 128, 4096], "ca_q": [2, 128, 4096],
                        "ca_k": [2, 128, 4096], "ca_v": [2, 128, 4096], "ca_o": [2, 128, 4096],
                        "ffn2_gu": [6, 128, 8192], "ffn2_dn": [6, 128, 4096],
                        "lnA_g": [3, D], "lnA_b": [3, D]})
        if part_b:
            ins.update({"ffn1_gu": [6, 128, 8192], "ffn1_dn": [6, 128, 4096],
                        "lnB_g": [1, D], "lnB_b": [1, D], "w_in": [6, 128, 4096]})
        for k, s in ins.items():
            self.d[k] = dt(k, s, F32, kind="ExternalInput")
        self.d["h_out"] = dt("h_out", [T, D], F32, kind="ExternalOutput")
        if part_b:
            self.d["zT"] = dt("zT", [2048, T], F32, kind="ExternalOutput")
            self.d["zt"] = dt("zt", [T, 1024], F32, kind="ExternalOutput")
        self.alloc()
        self.build()
        self.P.finish()
        self.P.emit()
        self.es.close()

    def sb(self, name, shape, dtype):
        return self.es.enter_context(self.nc.sbuf_tensor(name, shape, dtype))

    def alloc(self):
        nc = self.nc
        self.h = self.sb("h", [128, NT, D], F32)
        self.H = [Buf("h%d" % i) for i in range(NT)]
        self.hT = self.sb("hT", [128, 8, T], BF16)
        self.HT = [Buf("hT%d" % i) for i in range(NT)]
        self.gu = [self.sb("gu%d" % i, [128, 8192], BF16) for i in range(2)]
        self.GU = [Buf("gu%d" % i) for i in range(2)]
        self.wdb = [self.sb("wdb%d" % i, [128, 4096], BF16) for i in range(2)]
        self.WD = [Buf("wdb%d" % i) for i in range(2)]
        self.actb = [self.sb("actb%d" % i, [128, 4096], BF16) for i in range(2)]
        self.ACT = [Buf("actb%d" % i) for i in range(2)]
        self.lnG = self.sb("lnG", [128, D], F32)
        self.lnBt = self.sb("lnBt", [128, D], F32)
        self.LNP = Buf("lnp")
        self.ident = self.sb("identb", [128, 128], BF16)
        self.IDENT = Buf("ident")
        self.tmp = [self.sb("tmp%d" % i, [128, D], F32) for i in range(2)]
        self.TMP = [Buf("tmp%d" % i) for i in range(2)]
        self.sg = [self.sb("sg%d" % i, [128, 512], F32) for i in range(2)] * 2
        self.SG = [Buf("sg%d" % i) for i in range(2)] * 2
        self.hb = [self.sb("hb%d" % i, [128, D], BF16) for i in range(2)]
        self.HB = [Buf("hb%d" % i) for i in range(2)]
        self.st = [self.sb("st%d" % i, [128, 16], F32) for i in range(2)]
        self.ST = [Buf("st%d" % i) for i in range(2)]
        self.sm = [self.sb("sm%d" % i, [128, 8], F32) for i in range(2)]
        self.SM = [Buf("sm%d" % i) for i in range(2)]
        if self.part_a:
            self.ca_s = [self.sb("ca_s%d" % i, [128, 16], F32) for i in range(2)]
            self.CAS = [Buf("cas%d" % i) for i in range(2)]
            self.p = [self.sb("p%d" % i, [128, 1024], BF16) for i in range(1)] * 2
            self.PB_ = [Buf("p%d" % i) for i in range(1)] * 2
            self.pT = [self.sb("pT%d" % i, [128, 1024], BF16) for i in range(1)] * 2
            self.PT = [Buf("pT%d" % i) for i in range(1)] * 2
            self.oT = [self.sb("oT%d" % i, [128, 1024], BF16) for i in range(1)] * 2
            self.OT = [Buf("oT%d" % i) for i in range(1)] * 2
            self.qT = self.sb("qT", [128, 8, 512], BF16)
            self.QT = Buf("qT")
            self.kT = self.sb("kT", [128, 8, NMEM], BF16)
            self.KT = Buf("kT")
            self.vm = self.sb("vm", [128, 2, D], BF16)
            self.VM = Buf("vm")
            self.memT = self.qT[:, :, 0:NMEM]
            self.MEMT = self.QT
        self.ps = [self.es.enter_context(nc.psum_tensor("ps%d" % i, [128, 512], F32)) for i in range(7)]
        self.PS = [Buf("ps%d" % i) for i in range(7)]
        self.psb = self.es.enter_context(nc.psum_tensor("psb", [128, 1024], BF16))
        self.PSB = Buf("psb")
        self.lnidx = 0
        self.sgi = 0
        self.rot = 0

    def make_hT(self, tt):
        P = self.P
        i = tt % 2
        hb, HB = self.hb[i], self.HB[i]
        h = self.h
        P.op("act", lambda e: e.copy(out=hb[:, :], in_=h[:, tt, :]), reads=[self.H[tt]], writes=[HB])
        psb = self.psb
        ident = self.ident
        for k in range(8):
            P.op("pe", lambda e, k=k: e.transpose(out=psb[:, k * 128:(k + 1) * 128], in_=hb[:, k * 128:(k + 1) * 128],
                                                   identity=ident[:, :]),
                 reads=[HB, self.IDENT], writes=[self.PSB] if k in (0, 7) else [])
        hT = self.hT
        P.op("act", lambda e: e.copy(out=hT[:, :, tt * 128:(tt + 1) * 128],
                                     in_=psb[:, :].rearrange("p (k t) -> p k t", k=8)),
             reads=[self.PSB], writes=[self.HT[tt]])

    def load_ln(self, g_ap, b_ap):
        P = self.P
        lnG, lnBt = self.lnG, self.lnBt
        P.dma("sp", lambda e: e.dma_start(out=lnG[:, :], in_=g_ap.partition_broadcast(128)), writes=[self.LNP])
        P.dma("sp", lambda e: e.dma_start(out=lnBt[:, :], in_=b_ap.partition_broadcast(128)), writes=[self.LNP])

    def layer_norm_tile(self, tt, out_ap=None):
        P = self.P
        h = self.h
        i = self.rot % 2
        self.rot += 1
        st, ST, sm, SM = self.st[i], self.ST[i], self.sm[i], self.SM[i]
        tmp, TMP = self.tmp[i], self.TMP[i]
        Hb = self.H[tt]
        P.op("dve", lambda e: e.bn_stats(out=st[:, 0:6], in_=h[:, tt, 0:512]), reads=[Hb], writes=[ST])
        P.op("dve", lambda e: e.bn_stats(out=st[:, 6:12], in_=h[:, tt, 512:1024]), reads=[Hb, ST], writes=[ST])
        P.op("dve", lambda e: e.bn_aggr(out=sm[:, 0:2], in_=st[:, 0:12]), reads=[ST], writes=[SM])
        P.op("act", lambda e: e.activation(out=sm[:, 2:3], in_=sm[:, 1:2], func=AF.Sqrt, bias=self.eps_ln[:, 0:1], scale=1.0),
             reads=[SM, self.EPS], writes=[SM])
        P.op("dve", lambda e: e.reciprocal(out=sm[:, 3:4], in_=sm[:, 2:3]), reads=[SM], writes=[SM])
        lnG, lnBt = self.lnG, self.lnBt
        P.op("dve", lambda e: e.scalar_tensor_tensor(out=tmp[:, :], in0=h[:, tt, :], scalar=sm[:, 0:1], in1=lnG[:, :],
                                                      op0=ALU.subtract, op1=ALU.mult), reads=[SM, Hb, self.LNP], writes=[TMP])
        P.op("dve", lambda e: e.scalar_tensor_tensor(out=h[:, tt, :], in0=tmp[:, :], scalar=sm[:, 3:4], in1=lnBt[:, :],
                                                      op0=ALU.mult, op1=ALU.add), reads=[SM, TMP, self.LNP], writes=[Hb])
        if out_ap is not None:
            P.dma("sp", lambda e: e.dma_start(out=out_ap, in_=h[:, tt, :]), reads=[Hb], is_out=True)
        self.make_hT(tt)

    def wload(self, slot_t, SLOT, off, dram_ap, pat=None, **kw):
        n = dram_ap.shape[1]
        dst = slot_t[:, off:off + n].rearrange("p (a b) -> p a b", b=2048)
        src = dram_ap.rearrange("p (a b) -> p a b", b=2048)
        self.P.dma("pool", lambda e: e.dma_start(out=dst, in_=src), writes=[SLOT])

    def ffn(self, wg, wu, wd, ln_g, ln_b, out_ap_fn=None):
        P = self.P
        h, hT = self.h, self.hT
        groups = [list(range(i, min(i + 4, NFC))) for i in range(0, NFC, 4)]
        NG = len(groups)

        def load_gu(gi):
            s = gi % 2
            grp = groups[gi]
            c0, nc_ = grp[0] * 128, len(grp) * 128
            self.wload(self.gu[s], self.GU[s], 0, wg[gi, :, 0:8 * nc_])
            self.wload(self.gu[s], self.GU[s], 4096, wg[gi, :, 4096:4096 + 8 * nc_])

        def load_wd(gi):
            s = gi % 2
            grp = groups[gi]
            c0, nc_ = grp[0] * 128, len(grp) * 128
            self.wload(self.wdb[s], self.WD[s], 0, wd[gi, :, 0:len(grp) * 1024])

        self.load_ln(ln_g, ln_b)
        load_gu(0)
        load_wd(0)
        for tt in range(NT):
            P.op("act", lambda e, tt=tt: e.mul(out=h[:, tt, :], in_=h[:, tt, :], mul=ALPHA), reads=[self.H[tt]], writes=[self.H[tt]])

        def GUp(gi, th):
            s = gi % 2
            grp = groups[gi]
            ncol = len(grp) * 128
            gw = self.gu[s][:, 0:8 * ncol].rearrange("p (k c) -> p k c", k=8)
            uw = self.gu[s][:, 4096:4096 + 8 * ncol].rearrange("p (k c) -> p k c", k=8)
            a_i = (gi * 2 + th) % 2
            actv = self.actb[a_i][:, :].rearrange("p (f t) -> p f t", f=4)
            for fl in range(len(grp)):
                for st2 in range(2):
                    stg = th * 2 + st2
                    tok0 = stg * 512
                    pg_i = (fl * 2 + st2) % 2
                    pg, PG = self.ps[pg_i], self.PS[pg_i]
                    pu, PU = self.ps[2 + pg_i], self.PS[2 + pg_i]
                    rd = [self.GU[s]] + [self.HT[stg * 4 + j] for j in range(4)]
                    for k in range(8):
                        P.op("pe", lambda e, k=k, pg=pg, fl=fl, tok0=tok0, gw=gw: e.matmul(
                            out=pg[:, :], lhsT=gw[:, k, fl * 128:(fl + 1) * 128], rhs=hT[:, k, tok0:tok0 + 512],
                            start=(k == 0), stop=(k == 7)), reads=rd, writes=[PG] if k in (0, 7) else [])
                    for k in range(8):
                        P.op("pe", lambda e, k=k, pu=pu, fl=fl, tok0=tok0, uw=uw: e.matmul(
                            out=pu[:, :], lhsT=uw[:, k, fl * 128:(fl + 1) * 128], rhs=hT[:, k, tok0:tok0 + 512],
                            start=(k == 0), stop=(k == 7)), reads=rd, writes=[PU] if k in (0, 7) else [])
                    si = self.sgi % 3
                    self.sgi += 1
                    sgt, SGB = self.sg[si], self.SG[si]
                    P.op("act", lambda e, sgt=sgt, pg=pg: e.activation(out=sgt[:, :], in_=pg[:, :], func=AF.Silu),
                         reads=[PG], writes=[SGB])
                    P.op("dve", lambda e, sgt=sgt, pu=pu, fl=fl, st2=st2, actv=actv: e.tensor_tensor(
                        out=actv[:, fl, st2 * 512:(st2 + 1) * 512], in0=sgt[:, :], in1=pu[:, :], op=ALU.mult),
                        reads=[SGB, PU], writes=[self.ACT[a_i]])

        def Dn(gi, th):
            s = gi % 2
            grp = groups[gi]
            a_i = (gi * 2 + th) % 2
            actv = self.actb[a_i][:, :].rearrange("p (f t) -> p f t", f=4)
            wdv = self.wdb[s][:, :].rearrange("p (c d) -> p c d", c=4)
            for tl in range(8):
                tt = th * 8 + tl
                for dh in range(2):
                    pi = 4 + (tl * 2 + dh) % 2
                    pd, PD = self.ps[pi], self.PS[pi]
                    n = len(grp)
                    for fl in range(n):
                        P.op("pe", lambda e, fl=fl, pd=pd, tl=tl, dh=dh, n=n, actv=actv, wdv=wdv: e.matmul(
                            out=pd[:, :], lhsT=actv[:, fl, tl * 128:(tl + 1) * 128], rhs=wdv[:, fl, dh * 512:(dh + 1) * 512],
                            start=(fl == 0), stop=(fl == n - 1)), reads=[self.ACT[a_i], self.WD[s]],
                            writes=[PD] if fl in (0, n - 1) else [])
                    P.op("dve", lambda e, pd=pd, tt=tt, dh=dh: e.scalar_tensor_tensor(
                        out=h[:, tt, dh * 512:(dh + 1) * 512], in0=pd[:, :], scalar=0.5, in1=h[:, tt, dh * 512:(dh + 1) * 512],
                        op0=ALU.mult, op1=ALU.add), reads=[PD, self.H[tt]], writes=[self.H[tt]])
                if gi == NG - 1:
                    self.layer_norm_tile(tt, None if out_ap_fn is None else out_ap_fn(tt))

        steps = [(gi, th) for gi in range(NG) for th in range(2)]
        for i, (gi, th) in enumerate(steps):
            if th == 0 and gi + 1 < NG:
                load_gu(gi + 1)
            GUp(gi, th)
            if i >= 1:
                Dn(*steps[i - 1])
            if th == 0 and gi + 1 < NG:
                load_wd(gi + 1)
        Dn(*steps[-1])

    def w_in_phase(self, w_in):
        P = self.P
        hT = self.hT
        zT, zt = self.d["zT"], self.d["zt"]
        fm_chunks = list(range(0, 14)) + [22, 23]
        ngrp = 6

        def load(cg):
            s = cg % 2
            half = (cg // 2) % 2
            self.wload(self.gu[s], self.GU[s], half * 4096, w_in[cg, :, :])

        load(0)
        load(1)
        pi = 0
        for cg in range(ngrp):
            s = cg % 2
            half = (cg // 2) % 2
            wv = self.gu[s][:, half * 4096:(half + 1) * 4096].rearrange("p (k c) -> p k c", k=8)
            chunks = [cg * 4 + j for j in range(4)]
            fm = [c for c in chunks if c in fm_chunks]
            tm = [c for c in chunks if c not in fm_chunks]
            for c in fm:
                cl = c - cg * 4
                zi = fm_chunks.index(c)
                for stg in range(4):
                    ps, PSb = self.ps[pi % 7], self.PS[pi % 7]
                    pi += 1
                    rd = [self.GU[s]] + [self.HT[stg * 4 + j] for j in range(4)]
                    for k in range(8):
                        P.op("pe", lambda e, k=k, ps=ps, cl=cl, stg=stg, wv=wv: e.matmul(
                            out=ps[:, :], lhsT=wv[:, k, cl * 128:(cl + 1) * 128], rhs=hT[:, k, stg * 512:(stg + 1) * 512],
                            start=(k == 0), stop=(k == 7)), reads=rd, writes=[PSb] if k in (0, 7) else [])
                    si = self.sgi % 3
                    self.sgi += 1
                    sgt, SGB = self.sg[si], self.SG[si]
                    P.op("act", lambda e, sgt=sgt, ps=ps: e.copy(out=sgt[:, :], in_=ps[:, :]), reads=[PSb], writes=[SGB])
                    P.dma("sp", lambda e, sgt=sgt, zi=zi, stg=stg: e.dma_start(
                        out=zT[zi * 128:(zi + 1) * 128, stg * 512:(stg + 1) * 512], in_=sgt[:, :]), reads=[SGB], is_out=True)
            if tm:
                cl0 = tm[0] - cg * 4
                ncol = len(tm) * 128
                zc0 = (tm[0] - 14) * 128
                for tt in range(NT):
                    ps, PSb = self.ps[pi % 7], self.PS[pi % 7]
                    pi += 1
                    rd = [self.GU[s], self.HT[tt]]
                    for k in range(8):
                        P.op("pe", lambda e, k=k, ps=ps, tt=tt, cl0=cl0, ncol=ncol, wv=wv: e.matmul(
                            out=ps[:, 0:ncol], lhsT=hT[:, k, tt * 128:(tt + 1) * 128], rhs=wv[:, k, cl0 * 128:cl0 * 128 + ncol],
                            start=(k == 0), stop=(k == 7)), reads=rd, writes=[PSb] if k in (0, 7) else [])
                    si = self.sgi % 3
                    self.sgi += 1
                    sgt, SGB = self.sg[si], self.SG[si]
                    P.op("act", lambda e, sgt=sgt, ps=ps, ncol=ncol: e.copy(out=sgt[:, 0:ncol], in_=ps[:, 0:ncol]),
                         reads=[PSb], writes=[SGB])
                    P.dma("sp", lambda e, sgt=sgt, tt=tt, zc0=zc0, ncol=ncol: e.dma_start(
                        out=zt[tt * 128:(tt + 1) * 128, zc0:zc0 + ncol], in_=sgt[:, 0:ncol]), reads=[SGB], is_out=True)
            if cg + 2 < ngrp:
                load(cg + 2)


    def resid_linear(self, w_dram, slot_i, ln_row):
        P = self.P
        h, hT = self.h, self.hT
        d = self.d
        self.load_ln(d["lnA_g"][ln_row:ln_row + 1, :], d["lnA_b"][ln_row:ln_row + 1, :])
        for half in range(2):
            self.wload(self.gu[slot_i], self.GU[slot_i], half * 4096, w_dram[half, :, :])
        for tt in range(NT):
            P.op("act", lambda e, tt=tt: e.mul(out=h[:, tt, :], in_=h[:, tt, :], mul=ALPHA), reads=[self.H[tt]], writes=[self.H[tt]])
        pi = 0
        for half in range(2):
            wv = self.gu[slot_i][:, half * 4096:(half + 1) * 4096].rearrange("p (k c) -> p k c", k=8)
            for tt in range(NT):
                ps, PSb = self.ps[4 + pi % 2], self.PS[4 + pi % 2]
                pi += 1
                for k in range(8):
                    P.op("pe", lambda e, k=k, ps=ps, tt=tt, wv=wv: e.matmul(
                        out=ps[:, :], lhsT=hT[:, k, tt * 128:(tt + 1) * 128], rhs=wv[:, k, :], start=(k == 0), stop=(k == 7)),
                        reads=[self.GU[slot_i], self.HT[tt]], writes=[PSb] if k in (0, 7) else [])
                P.op("dve", lambda e, ps=ps, tt=tt, half=half: e.tensor_tensor(
                    out=h[:, tt, half * 512:(half + 1) * 512], in0=ps[:, :], in1=h[:, tt, half * 512:(half + 1) * 512], op=ALU.add),
                    reads=[PSb, self.H[tt]], writes=[self.H[tt]])
                if half == 1:
                    self.layer_norm_tile(tt)

    def cross_attn(self):
        P = self.P
        d = self.d
        h, hT = self.h, self.hT
        psb, ident = self.psb, self.ident
        for half in range(2):
            self.wload(self.gu[1], self.GU[1], half * 4096, d["ca_q"][half, :, :])
            self.wload(self.wdb[half], self.WD[half], 0, d["ca_k"][half, :, :])
            self.wload(self.actb[half], self.ACT[half], 0, d["ca_v"][half, :, :])
        for half in range(2):
            self.wload(self.gu[0], self.GU[0], half * 4096, d["ca_o"][half, :, :])
        wq = [self.gu[1][:, hf * 4096:(hf + 1) * 4096].rearrange("p (k c) -> p k c", k=8) for hf in range(2)]
        wo = [self.gu[0][:, hf * 4096:(hf + 1) * 4096].rearrange("p (k c) -> p k c", k=8) for hf in range(2)]
        wk = [self.wdb[hf][:, :].rearrange("p (k c) -> p k c", k=8) for hf in range(2)]
        wv = [self.actb[hf][:, :].rearrange("p (k c) -> p k c", k=8) for hf in range(2)]
        memT, kT, vm, qT = self.memT, self.kT, self.vm, self.qT
        for mt in range(2):
            hb, HB = self.hb[mt], self.HB[mt]
            P.dma("pool", lambda e, mt=mt, hb=hb: e.dma_start(out=hb[:, :], in_=d["mem"][mt * 128:(mt + 1) * 128, :]), writes=[HB])
            for k in range(8):
                P.op("pe", lambda e, k=k, hb=hb: e.transpose(out=psb[:, k * 128:(k + 1) * 128], in_=hb[:, k * 128:(k + 1) * 128],
                                                              identity=ident[:, :]),
                     reads=[HB, self.IDENT], writes=[self.PSB] if k in (0, 7) else [])
            P.op("act", lambda e, mt=mt: e.copy(out=memT[:, :, mt * 128:(mt + 1) * 128],
                                                in_=psb[:, :].rearrange("p (k t) -> p k t", k=8)),
                 reads=[self.PSB], writes=[self.MEMT])
        ps6, PS6 = self.ps[6], self.PS[6]
        for hdc in range(8):
            hf, cl = hdc // 4, hdc % 4
            for k in range(8):
                P.op("pe", lambda e, k=k, hf=hf, cl=cl: e.matmul(out=ps6[:, 0:NMEM], lhsT=wk[hf][:, k, cl * 128:(cl + 1) * 128],
                                                                  rhs=memT[:, k, :], start=(k == 0), stop=(k == 7)),
                     reads=[self.WD[hf], self.MEMT], writes=[PS6] if k in (0, 7) else [])
            P.op("act", lambda e, hdc=hdc: e.copy(out=kT[:, hdc, :], in_=ps6[:, 0:NMEM]), reads=[PS6], writes=[self.KT])
        for mt in range(2):
            for hf in range(2):
                for k in range(8):
                    P.op("pe", lambda e, k=k, hf=hf, mt=mt: e.matmul(out=ps6[:, :], lhsT=memT[:, k, mt * 128:(mt + 1) * 128],
                                                                      rhs=wv[hf][:, k, :], start=(k == 0), stop=(k == 7)),
                         reads=[self.ACT[hf], self.MEMT], writes=[PS6] if k in (0, 7) else [])
                P.op("act", lambda e, mt=mt, hf=hf: e.copy(out=vm[:, mt, hf * 512:(hf + 1) * 512], in_=ps6[:, :]),
                     reads=[PS6], writes=[self.VM])
        self.load_ln(d["lnA_g"][1:2, :], d["lnA_b"][1:2, :])
        cas = self.ca_s
        for stg in range(4):
            for hdc in range(8):
                hf, cl = hdc // 4, hdc % 4
                rd = [self.GU[1]] + [self.HT[stg * 4 + j] for j in range(4)]
                for k in range(8):
                    P.op("pe", lambda e, k=k, hf=hf, cl=cl, stg=stg: e.matmul(
                        out=ps6[:, :], lhsT=wq[hf][:, k, cl * 128:(cl + 1) * 128], rhs=hT[:, k, stg * 512:(stg + 1) * 512],
                        start=(k == 0), stop=(k == 7)), reads=rd, writes=[PS6] if k in (0, 7) else [])
                P.op("act", lambda e, hdc=hdc: e.copy(out=qT[:, hdc, :], in_=ps6[:, :]), reads=[PS6], writes=[self.QT])
            for tl in range(4):
                tt = stg * 4 + tl
                i = tt % 2
                cs_, CS = cas[i], self.CAS[i]
                p, PBf, pT, PTb, oT, OTb = self.p[i], self.PB_[i], self.pT[i], self.PT[i], self.oT[i], self.OT[i]
                P.op("act", lambda e, tt=tt: e.mul(out=h[:, tt, :], in_=h[:, tt, :], mul=ALPHA), reads=[self.H[tt]], writes=[self.H[tt]])
                for hh in range(4):
                    pss, PSS = self.ps[hh // 2], self.PS[hh // 2]
                    for hc in range(2):
                        P.op("pe", lambda e, hh=hh, hc=hc, tl=tl, pss=pss: e.matmul(
                            out=pss[:, (hh % 2) * 256:(hh % 2 + 1) * 256], lhsT=qT[:, hh * 2 + hc, tl * 128:(tl + 1) * 128],
                            rhs=kT[:, hh * 2 + hc, :], start=(hc == 0), stop=(hc == 1)),
                            reads=[self.QT, self.KT], writes=[PSS] if (hh % 2 == 0 and hc == 0) or (hh % 2 == 1 and hc == 1) else [])
                for j in range(2):
                    P.op("dve", lambda e, j=j, cs_=cs_: e.tensor_reduce(
                        out=cs_[:, 2 * j:2 * j + 2], in_=self.ps[j][:, :].rearrange("p (a b) -> p a b", a=2), axis=AX.X, op=ALU.max),
                        reads=[self.PS[j]], writes=[CS])
                P.op("dve", lambda e, cs_=cs_: e.tensor_scalar(out=cs_[:, 4:8], in0=cs_[:, 0:4], scalar1=-1.0 / 16.0, scalar2=None,
                                                             op0=ALU.mult), reads=[CS], writes=[CS])
                for hh in range(4):
                    P.op("act", lambda e, hh=hh, cs_=cs_, p=p: e.activation(
                        out=p[:, hh * 256:(hh + 1) * 256], in_=self.ps[hh // 2][:, (hh % 2) * 256:(hh % 2 + 1) * 256], func=AF.Exp,
                        bias=cs_[:, 4 + hh:5 + hh], scale=1.0 / 16.0, accum_out=cs_[:, 8 + hh:9 + hh]),
                        reads=[self.PS[hh // 2], CS], writes=[PBf, CS])
                P.op("dve", lambda e, cs_=cs_: e.reciprocal(out=cs_[:, 12:16], in_=cs_[:, 8:12]), reads=[CS], writes=[CS])
                for hh in range(4):
                    P.op("dve", lambda e, hh=hh, cs_=cs_, p=p: e.tensor_scalar(
                        out=p[:, hh * 256:(hh + 1) * 256], in0=p[:, hh * 256:(hh + 1) * 256], scalar1=cs_[:, 12 + hh:13 + hh],
                        scalar2=None, op0=ALU.mult), reads=[CS, PBf], writes=[PBf])
                for j in range(8):
                    P.op("pe", lambda e, j=j, p=p: e.transpose(out=psb[:, j * 128:(j + 1) * 128], in_=p[:, j * 128:(j + 1) * 128],
                                                                identity=ident[:, :]),
                         reads=[PBf, self.IDENT], writes=[self.PSB] if j in (0, 7) else [])
                P.op("act", lambda e, pT=pT: e.copy(out=pT[:, :], in_=psb[:, :]), reads=[self.PSB], writes=[PTb])
                for hdc in range(8):
                    hh, c = hdc // 2, hdc % 2
                    pso, PSO = self.ps[2 + hdc // 4], self.PS[2 + hdc // 4]
                    for mc in range(2):
                        P.op("pe", lambda e, hdc=hdc, hh=hh, c=c, mc=mc, pso=pso, pT=pT: e.matmul(
                            out=pso[:, (hdc % 4) * 128:(hdc % 4 + 1) * 128], lhsT=vm[:, mc, hh * 256 + c * 128:hh * 256 + (c + 1) * 128],
                            rhs=pT[:, (hh * 2 + mc) * 128:(hh * 2 + mc + 1) * 128], start=(mc == 0), stop=(mc == 1)),
                            reads=[self.VM, PTb], writes=[PSO] if (hdc % 4 == 0 and mc == 0) or (hdc % 4 == 3 and mc == 1) else [])
                for j in range(2):
                    P.op("act", lambda e, j=j, oT=oT: e.copy(out=oT[:, j * 512:(j + 1) * 512], in_=self.ps[2 + j][:, :]),
                         reads=[self.PS[2 + j]], writes=[OTb])
                for half in range(2):
                    psw, PSW = self.ps[4 + half], self.PS[4 + half]
                    for k in range(8):
                        P.op("pe", lambda e, k=k, half=half, psw=psw, oT=oT: e.matmul(
                            out=psw[:, :], lhsT=oT[:, k * 128:(k + 1) * 128], rhs=wo[half][:, k, :], start=(k == 0), stop=(k == 7)),
                            reads=[OTb, self.GU[0]], writes=[PSW] if k in (0, 7) else [])
                    P.op("dve", lambda e, tt=tt, half=half, psw=psw: e.tensor_tensor(
                        out=h[:, tt, half * 512:(half + 1) * 512], in0=psw[:, :], in1=h[:, tt, half * 512:(half + 1) * 512], op=ALU.add),
                        reads=[PSW, self.H[tt]], writes=[self.H[tt]])
                self.layer_norm_tile(tt)

    def build_a(self):
        P = self.P
        d = self.d
        hT = self.hT
        for k in range(8):
            for half in range(2):
                P.dma("pool", lambda e, k=k, half=half: e.dma_start(
                    out=hT[:, k, half * 1024:(half + 1) * 1024], in_=d["mixT"][k * 128:(k + 1) * 128, half * 1024:(half + 1) * 1024]),
                    writes=[self.HT[half * 8 + j] for j in range(8)])
        self.resid_linear(d["w_out"], 0, 0)
        self.cross_attn()
        last = not self.part_b
        self.ffn(d["ffn2_gu"], None, d["ffn2_dn"], d["lnA_g"][2:3, :], d["lnA_b"][2:3, :],
                 out_ap_fn=(lambda tt: d["h_out"][tt * 128:(tt + 1) * 128, :]) if last else None)

    def build(self):
        P = self.P
        d = self.d
        nc = self.nc
        h = self.h
        self.eps_ln = self.sb("eps_ln", [128, 2], F32)
        self.EPS = Buf("eps")
        eps_ln = self.eps_ln
        P.op("pool", lambda e: e.memset(eps_ln[:, 0:1], LN_EPS), writes=[self.EPS])
        P.op("pool", lambda e: e.memset(eps_ln[:, 1:2], RMS_EPS), writes=[self.EPS])
        ident = self.ident
        P.dma("pool", lambda e: e.dma_start(out=ident[:, :], in_=d["ident"][:, :]), writes=[self.IDENT])
        for tt in range(NT):
            P.dma("sp", lambda e, tt=tt: e.dma_start(out=h[:, tt, :], in_=d["h_in"][tt * 128:(tt + 1) * 128, :]),
                  writes=[self.H[tt]])
        if self.part_a:
            self.build_a()
        else:
            for tt in range(NT):
                self.make_hT(tt)
        if self.part_b:
            self.ffn(d["ffn1_gu"], None, d["ffn1_dn"], d["lnB_g"][0:1, :], d["lnB_b"][0:1, :],
                     out_ap_fn=lambda tt: d["h_out"][tt * 128:(tt + 1) * 128, :])
            self.w_in_phase(d["w_in"])


_CACHE = {}


def get_P(part_a, part_b):
    key = ("P", part_a, part_b)
    if key not in _CACHE:
        _CACHE[key] = PBuild(part_a, part_b).nc
    return _CACHE[key]


class MBuild:
    def __init__(self):
        nc = self.nc = bass.Bass("TRN2", target_bir_lowering=False)
        P = self.P = Prog(nc)
        es = self.es = contextlib.ExitStack()
        dt = nc.dram_tensor
        d = {}
        ins = {"zq": [128, SEQ], "zf": [128, SEQ], "zv": [SEQ, 128], "zg": [SEQ, 128], "lb2": [128, 2], "lsel": [128, 1],
               "ng": [1, 128], "zc": [3, 256, T + 2], "cw": [256, 3], "zp": [256, T + 15], "invc": [256, 16],
               "invw": [128, 2], "pw": [4, 64, 64], "pscale": [128, 2], "ident": [128, 128], "cmask": [128, 512],
               "amask": [128, 128]}
        for k, s in ins.items():
            d[k] = dt(k, s, F32, kind="ExternalInput")
        d["y_rec"] = dt("y_rec", [SEQ, 128], F32, kind="ExternalOutput")
        d["y_cp"] = dt("y_cp", [512, T], F32, kind="ExternalOutput")

        def sb(name, shape, dtype=F32):
            return es.enter_context(nc.sbuf_tensor("sb_" + name, shape, dtype))

        def ps_(name, shape, dtype=F32):
            return es.enter_context(nc.psum_tensor("ps_" + name, shape, dtype))

        CONST = Buf("const")
        ident = sb("ident", [128, 128], BF16)
        cmask = sb("cmask", [128, 512])
        amask = sb("amask", [128, 128])
        lb2 = sb("lb2", [128, 2])
        lsel = sb("lsel", [128, 1])
        lbc = sb("lbc", [128, 4])
        ngB = sb("ngB", [128, 128])
        P.dma("pool", lambda e: e.dma_start(out=ident[:, :], in_=d["ident"][:, :]), writes=[CONST])
        P.dma("sp", lambda e: e.dma_start(out=cmask[:, :], in_=d["cmask"][:, :]), writes=[CONST])
        P.dma("sp", lambda e: e.dma_start(out=amask[:, :], in_=d["amask"][:, :]), writes=[CONST])
        P.dma("sp", lambda e: e.dma_start(out=lb2[:, :], in_=d["lb2"][:, :]), writes=[CONST])
        P.dma("sp", lambda e: e.dma_start(out=lsel[:, :], in_=d["lsel"][:, :]), writes=[CONST])
        P.dma("sp", lambda e: e.dma_start(out=ngB[:, :], in_=d["ng"][0:1, :].partition_broadcast(128)), writes=[CONST])
        P.op("dve", lambda e: e.tensor_tensor(out=lbc[:, 0:1], in0=lb2[:, 1:2], in1=lb2[:, 0:1], op=ALU.subtract),
             reads=[CONST], writes=[CONST])
        P.op("act", lambda e: e.activation(out=lbc[:, 0:1], in_=lbc[:, 0:1], func=AF.Sigmoid), reads=[CONST], writes=[CONST])
        P.op("dve", lambda e: e.tensor_tensor(out=lbc[:, 0:1], in0=lbc[:, 0:1], in1=lsel[:, 0:1], op=ALU.mult),
             reads=[CONST], writes=[CONST])
        P.op("dve", lambda e: e.tensor_scalar(out=lbc[:, 1:2], in0=lbc[:, 0:1], scalar1=-1.0, scalar2=1.0, op0=ALU.mult, op1=ALU.add),
             reads=[CONST], writes=[CONST])
        P.op("dve", lambda e: e.tensor_scalar(out=lbc[:, 2:3], in0=lbc[:, 1:2], scalar1=-1.0, scalar2=None, op0=ALU.mult),
             reads=[CONST], writes=[CONST])
        P.op("pool", lambda e: e.memset(lbc[:, 3:4], RMS_EPS), reads=[CONST], writes=[CONST])

        psb = ps_("psb", [128, 1024], BF16)
        PSB = Buf("psb")
        psm = ps_("psm", [128, 512])
        PSM = Buf("psm")

        cb = sb("cb", [128, T + 2]); cc = sb("cc", [128, T + 2]); ch = sb("ch", [128, T + 2])
        acc = sb("acc", [128, T]); t1 = sb("t1", [128, T])
        cwt = sb("cwt", [128, 2, 3])
        CV = Buf("conv")
        P.dma("sp", lambda e: e.dma_start(out=cwt[:, :, :], in_=d["cw"].ap().rearrange("(c p) k -> p c k", p=128)), writes=[CONST])
        for ci in range(2):
            rs = slice(ci * 128, (ci + 1) * 128)
            for j, tl in enumerate((cb, cc, ch)):
                P.dma("sp", lambda e, j=j, tl=tl, rs=rs: e.dma_start(out=tl[:, :], in_=d["zc"][j, rs, :]), writes=[CV])
            P.op("pool", lambda e: e.tensor_tensor(out=cc[:, :], in0=cc[:, :], in1=ch[:, :], op=ALU.mult), reads=[CV], writes=[CV])
            P.op("pool", lambda e, ci=ci: e.tensor_scalar(out=acc[:, :], in0=cc[:, 2:T + 2], scalar1=cwt[:, ci, 2:3], scalar2=None,
                                                         op0=ALU.mult), reads=[CV, CONST], writes=[CV])
            for kk_ in (1, 0):
                P.op("pool", lambda e, ci=ci, kk_=kk_: e.tensor_scalar(out=t1[:, :], in0=cc[:, kk_:T + kk_], scalar1=cwt[:, ci, kk_:kk_ + 1],
                                                                     scalar2=None, op0=ALU.mult), reads=[CV, CONST], writes=[CV])
                P.op("pool", lambda e: e.tensor_tensor(out=acc[:, :], in0=acc[:, :], in1=t1[:, :], op=ALU.add), reads=[CV], writes=[CV])
            P.op("pool", lambda e: e.tensor_tensor(out=acc[:, :], in0=acc[:, :], in1=cb[:, 2:T + 2], op=ALU.mult), reads=[CV], writes=[CV])
            P.dma("sp", lambda e, rs=rs: e.dma_start(out=d["y_cp"][rs, :], in_=acc[:, :]), reads=[CV], is_out=True)

        L = T + 15
        pu = sb("pu", [128, L]); sA = sb("sA", [128, L]); sB = sb("sB", [128, L])
        pp = sb("pp", [128, T], BF16)
        invc = sb("invc", [128, 2, 16]); invw = sb("invw", [128, 2]); psc = sb("psc", [128, 2])
        pwf = sb("pwf", [128, 2, 128]); pwb = sb("pwb", [128, 2, 128], BF16)
        yst = [sb("yst%d" % i, [128, 512]) for i in range(2)]
        YST = [Buf("yst%d" % i) for i in range(2)]
        PL = Buf("pool")
        P.dma("sp", lambda e: e.dma_start(out=invc[:, :, :], in_=d["invc"].ap().rearrange("(c p) k -> p c k", p=128)), writes=[CONST])
        P.dma("sp", lambda e: e.dma_start(out=invw[:, :], in_=d["invw"][:, :]), writes=[CONST])
        P.dma("sp", lambda e: e.dma_start(out=psc[:, :], in_=d["pscale"][:, :]), writes=[CONST])
        PW = Buf("pw")
        P.op("pool", lambda e: e.memset(pwf[:, :, :], 0.0), writes=[PW])
        for g in range(4):
            ci, hf = g // 2, g % 2
            P.dma("sp", lambda e, g=g, ci=ci, hf=hf: e.dma_start(out=pwf[hf * 64:(hf + 1) * 64, ci, hf * 64:(hf + 1) * 64],
                                                               in_=d["pw"][g, :, :]), writes=[PW])
        P.op("pool", lambda e: e.tensor_copy(out=pwb[:, :, :], in_=pwf[:, :, :]), reads=[PW], writes=[PW])
        for ci in range(2):
            rs = slice(ci * 128, (ci + 1) * 128)
            P.dma("sp", lambda e, rs=rs: e.dma_start(out=pu[:, :], in_=d["zp"][rs, :]), writes=[PL])
            P.op("dve", lambda e: e.tensor_tensor(out=sA[:, 1:L], in0=pu[:, 1:L], in1=pu[:, 0:L - 1], op=ALU.add), reads=[PL], writes=[PL])
            if ci == 0:
                P.op("dve", lambda e: e.tensor_tensor(out=sB[64:128, 3:L], in0=sA[64:128, 3:L], in1=sA[64:128, 1:L - 2], op=ALU.add),
                     reads=[PL], writes=[PL])
                P.op("dve", lambda e: e.tensor_copy(out=sB[0:64, 15:L], in_=sA[0:64, 15:L]), reads=[PL], writes=[PL])
                win = sB
            else:
                P.op("dve", lambda e: e.tensor_tensor(out=sB[:, 3:L], in0=sA[:, 3:L], in1=sA[:, 1:L - 2], op=ALU.add), reads=[PL], writes=[PL])
                P.op("dve", lambda e: e.tensor_tensor(out=sA[:, 7:L], in0=sB[:, 7:L], in1=sB[:, 3:L - 4], op=ALU.add), reads=[PL], writes=[PL])
                P.op("dve", lambda e: e.tensor_tensor(out=sB[64:128, 15:L], in0=sA[64:128, 15:L], in1=sA[64:128, 7:L - 8], op=ALU.add),
                     reads=[PL], writes=[PL])
                P.op("dve", lambda e: e.tensor_copy(out=sB[0:64, 15:L], in_=sA[0:64, 15:L]), reads=[PL], writes=[PL])
                win = sB
            P.op("dve", lambda e, ci=ci, win=win: e.scalar_tensor_tensor(out=pp[:, 16:T], in0=win[:, 31:L], scalar=invw[:, ci:ci + 1],
                                                                       in1=pu[:, 31:L], op0=ALU.mult, op1=ALU.subtract),
                 reads=[PL, CONST], writes=[PL])
            P.op("dve", lambda e, ci=ci, win=win: e.tensor_tensor(out=sA[:, 0:16], in0=win[:, 15:31], in1=invc[:, ci, :], op=ALU.mult),
                 reads=[PL, CONST], writes=[PL])
            P.op("dve", lambda e: e.tensor_tensor(out=pp[:, 0:16], in0=sA[:, 0:16], in1=pu[:, 15:31], op=ALU.subtract), reads=[PL], writes=[PL])
            for sg in range(4):
                P.op("pe", lambda e, ci=ci, sg=sg: e.matmul(out=psm[:, :], lhsT=pwb[:, ci, :], rhs=pp[:, sg * 512:(sg + 1) * 512],
                                                            start=True, stop=True), reads=[PL, PW], writes=[PSM])
                yi = sg % 2
                P.op("act", lambda e, ci=ci, yi=yi: e.activation(out=yst[yi][:, :], in_=psm[:, :], func=AF.Identity, bias=0.0, scale=psc[:, ci:ci + 1]),
                     reads=[PSM, CONST], writes=[YST[yi]])
                P.dma("sp", lambda e, ci=ci, sg=sg, yi=yi: e.dma_start(out=d["y_cp"][256 + ci * 128:256 + (ci + 1) * 128, sg * 512:(sg + 1) * 512],
                                                                     in_=yst[yi][:, :]), reads=[YST[yi]], is_out=True)

        NB = SEQ // 512
        NS = 2
        W = {}
        for nm, shape, dty in (("q", [128, 512], F32), ("f", [128, 512], F32), ("v", [128, 4, 128], F32), ("g", [128, 4, 128], F32),
                               ("sig", [128, 512], F32), ("logf", [128, 512], F32), ("kk", [128, 512], F32), ("b", [128, 512], F32),
                               ("e1", [128, 512], F32), ("e2", [128, 512], F32), ("qt", [128, 512], BF16), ("ktb", [128, 512], BF16),
                               ("kh", [128, 512], BF16), ("vb", [128, 4, 128], BF16), ("qA", [128, 4, 128], BF16),
                               ("qB", [128, 4, 128], BF16), ("khA", [128, 4, 128], BF16), ("khB", [128, 4, 128], BF16),
                               ("at", [128, 4, 128], BF16), ("sq", [128, 128], F32), ("ss", [128, 12], F32),
                               ("y1", [128, 4, 128], F32), ("sgg", [128, 4, 128], F32)):
            W[nm] = [sb("w_%s%d" % (nm, i), shape, dty) for i in range(NS)]
            W[nm.upper() + "_"] = [Buf("w_%s%d" % (nm, i)) for i in range(NS)]
        a_all = sb("a_all", [128, SEQ // 64])
        AALL = Buf("a_all")
        Sb = sb("Sb", [128, (SEQ // 64) * 128], BF16)
        SB_ = Buf("Sb")
        Sf = [sb("Sf%d" % i, [128, 128]) for i in range(2)]
        SF = [Buf("Sf%d" % i) for i in range(2)]
        psU = [[ps_("psU%d%d" % (i, j), [128, 512]) for j in range(2)] for i in range(2)]
        PSU = [[Buf("psU%d%d" % (i, j)) for j in range(2)] for i in range(2)]
        psA = ps_("psA", [128, 512]); PSA = Buf("psA")
        psO = ps_("psO", [128, 512]); PSO = Buf("psO")
        for i in range(NS):
            for nm in ("qA", "qB", "khA", "khB"):
                P.op("pool", lambda e, nm=nm, i=i: e.memset(W[nm][i][:, :, :], 0.0), writes=[W[nm.upper() + "_"][i]])
        P.op("pool", lambda e: e.memset(Sf[0][:, :], 0.0), writes=[SF[0]])
        P.op("pool", lambda e: e.memset(Sb[:, 0:128], 0.0), writes=[SB_])
        for blk in range(NB):
            s = blk % NS
            c0 = blk * 512
            w = {k: v[s] for k, v in W.items()}
            q, f, v, g = w["q"], w["f"], w["v"], w["g"]
            P.dma("sp", lambda e, q=q, c0=c0: e.dma_start(out=q[:, :], in_=d["zq"][:, c0:c0 + 512]), writes=[w["Q_"]])
            P.dma("sp", lambda e, f=f, c0=c0: e.dma_start(out=f[:, :], in_=d["zf"][:, c0:c0 + 512]), writes=[w["F_"]])
            P.dma("sp", lambda e, v=v, c0=c0: e.dma_start(out=v[:, :, :], in_=d["zv"][c0:c0 + 512, :].rearrange("(t p) v -> p t v", p=128)),
                  writes=[w["V_"]])
            P.dma("sp", lambda e, g=g, c0=c0: e.dma_start(out=g[:, :, :], in_=d["zg"][c0:c0 + 512, :].rearrange("(t p) v -> p t v", p=128)),
                  writes=[w["G_"]])
            sig, logf, kk, b, e1, e2 = w["sig"], w["logf"], w["kk"], w["b"], w["e1"], w["e2"]
            qt, ktb, kh, vb, qA, qB, khA, khB, at = w["qt"], w["ktb"], w["kh"], w["vb"], w["qA"], w["qB"], w["khA"], w["khB"], w["at"]
            P.op("act", lambda e, sig=sig, f=f: e.activation(out=sig[:, :], in_=f[:, :], func=AF.Sigmoid), reads=[w["F_"]], writes=[w["SIG_"]])
            P.op("act", lambda e, sig=sig, logf=logf: e.activation(out=logf[:, :], in_=sig[:, :], func=AF.Ln, bias=lbc[:, 0:1], scale=lbc[:, 1:2]),
                 reads=[w["SIG_"], CONST], writes=[w["LOGF_"]])
            P.op("dve", lambda e, sig=sig, kk=kk: e.tensor_scalar(out=kk[:, :], in0=sig[:, :], scalar1=lbc[:, 2:3], scalar2=lbc[:, 1:2],
                                                                op0=ALU.mult, op1=ALU.add), reads=[w["SIG_"], CONST], writes=[w["KK_"]])
            P.op("dve", lambda e, b=b, logf=logf: e.tensor_tensor_scan(out=b[:, :], data0=cmask[:, :], data1=logf[:, :], initial=0.0,
                                                                     op0=ALU.mult, op1=ALU.add), reads=[w["LOGF_"], CONST], writes=[w["B_"]])
            P.op("act", lambda e, b=b, e1=e1: e.activation(out=e1[:, :], in_=b[:, :], func=AF.Exp), reads=[w["B_"]], writes=[w["E1_"]])
            P.op("act", lambda e, b=b, e2=e2: e.activation(out=e2[:, :], in_=b[:, :], func=AF.Exp, scale=-1.0), reads=[w["B_"]], writes=[w["E2_"]])
            P.op("act", lambda e, b=b, blk=blk: e.activation(out=a_all[:, blk * 8:(blk + 1) * 8],
                                                             in_=b[:, :].rearrange("p (c t) -> p c t", t=64)[:, :, 63], func=AF.Exp),
                 reads=[w["B_"]], writes=[AALL])
            P.op("pool", lambda e, v=v, vb=vb: e.tensor_copy(out=vb[:, :, :], in_=v[:, :, :]), reads=[w["V_"]], writes=[w["VB_"]])
            P.op("dve", lambda e, q=q, e1=e1, qt=qt: e.tensor_tensor(out=qt[:, :], in0=q[:, :], in1=e1[:, :], op=ALU.mult),
                 reads=[w["Q_"], w["E1_"]], writes=[w["QT_"]])
            P.op("dve", lambda e, kk=kk, e2=e2: e.tensor_tensor(out=e2[:, :], in0=kk[:, :], in1=e2[:, :], op=ALU.mult),
                 reads=[w["KK_"], w["E2_"]], writes=[w["E2_"]])
            P.op("pool", lambda e, e2=e2, ktb=ktb: e.tensor_copy(out=ktb[:, :], in_=e2[:, :]), reads=[w["E2_"]], writes=[w["KTB_"]])
            P.op("dve", lambda e, e2=e2, kh=kh, blk=blk: e.tensor_tensor(
                out=kh[:, :].rearrange("p (c t) -> p c t", t=64), in0=e2[:, :].rearrange("p (c t) -> p c t", t=64),
                in1=a_all[:, blk * 8:(blk + 1) * 8].unsqueeze(2).to_broadcast([128, 8, 64]), op=ALU.mult),
                reads=[w["E2_"], AALL], writes=[w["KH_"]])
            qt3 = qt[:, :].rearrange("p (j t) -> p j t", t=128)
            P.op("pool", lambda e, qA=qA, qt3=qt3: e.tensor_copy(out=qA[:, :, 0:64], in_=qt3[:, :, 0:64]), reads=[w["QT_"]], writes=[w["QA_"]])
            P.op("pool", lambda e, qB=qB, qt3=qt3: e.tensor_copy(out=qB[:, :, 64:128], in_=qt3[:, :, 64:128]), reads=[w["QT_"]], writes=[w["QB_"]])
            for j in range(4):
                P.op("pe", lambda e, j=j, kh=kh: e.transpose(out=psb[:, j * 128:(j + 1) * 128], in_=kh[:, j * 128:(j + 1) * 128], identity=ident[:, :]),
                     reads=[w["KH_"], CONST], writes=[PSB] if j in (0, 3) else [])
            psb3 = psb[:, 0:512].rearrange("p (j t) -> p j t", t=128)
            P.op("act", lambda e, khA=khA, psb3=psb3: e.copy(out=khA[0:64, :, :], in_=psb3[0:64, :, :]), reads=[PSB], writes=[w["KHA_"]])
            P.op("act", lambda e, khB=khB, psb3=psb3: e.copy(out=khB[64:128, :, :], in_=psb3[64:128, :, :]), reads=[PSB], writes=[w["KHB_"]])
            pu_, PU_ = psU[s], PSU[s]
            for j in range(4):
                for hf, kx, KX in ((0, khA, w["KHA_"]), (1, khB, w["KHB_"])):
                    cl = 2 * j + hf
                    P.op("pe", lambda e, j=j, kx=kx, cl=cl, pu_=pu_, vb=vb: e.matmul(
                        out=pu_[cl // 4][:, (cl % 4) * 128:(cl % 4 + 1) * 128], lhsT=kx[:, j, :], rhs=vb[:, j, :], start=True, stop=True),
                        reads=[KX, w["VB_"]], writes=[PU_[cl // 4]])
            for j in range(4):
                P.op("pe", lambda e, j=j, ktb=ktb, qt=qt: e.matmul(out=psA[:, j * 128:(j + 1) * 128], lhsT=ktb[:, j * 128:(j + 1) * 128],
                                                                   rhs=qt[:, j * 128:(j + 1) * 128], start=True, stop=True),
                     reads=[w["KTB_"], w["QT_"]], writes=[PSA] if j in (0, 3) else [])
            P.op("dve", lambda e, at=at: e.tensor_tensor(out=at[:, :, :], in0=psA[:, :].rearrange("p (j t) -> p j t", t=128),
                                                         in1=amask[:, :].unsqueeze(1).to_broadcast([128, 4, 128]), op=ALU.mult),
                 reads=[PSA, CONST], writes=[w["AT_"]])
            for cl in range(8):
                c = blk * 8 + cl
                so, sn = Sf[c % 2], Sf[(c + 1) % 2]
                SO, SN = SF[c % 2], SF[(c + 1) % 2]
                if c + 1 < SEQ // 64:
                    P.op("dve", lambda e, so=so, sn=sn, c=c, cl=cl, pu_=pu_: e.scalar_tensor_tensor(
                        out=sn[:, :], in0=so[:, :], scalar=a_all[:, c:c + 1], in1=pu_[cl // 4][:, (cl % 4) * 128:(cl % 4 + 1) * 128],
                        op0=ALU.mult, op1=ALU.add), reads=[SO, AALL, PU_[cl // 4]], writes=[SN])
                    P.op("act", lambda e, sn=sn, c=c: e.copy(out=Sb[:, (c + 1) * 128:(c + 2) * 128], in_=sn[:, :]), reads=[SN], writes=[SB_])
            for j in range(4):
                cA = blk * 8 + 2 * j
                P.op("pe", lambda e, j=j, at=at, vb=vb: e.matmul(out=psO[:, j * 128:(j + 1) * 128], lhsT=at[:, j, :], rhs=vb[:, j, :],
                                                                 start=True, stop=False), reads=[w["AT_"], w["VB_"]], writes=[PSO] if j == 0 else [])
                P.op("pe", lambda e, j=j, qA=qA, cA=cA: e.matmul(out=psO[:, j * 128:(j + 1) * 128], lhsT=qA[:, j, :], rhs=Sb[:, cA * 128:(cA + 1) * 128],
                                                                 start=False, stop=False), reads=[w["QA_"], SB_])
                P.op("pe", lambda e, j=j, qB=qB, cA=cA: e.matmul(out=psO[:, j * 128:(j + 1) * 128], lhsT=qB[:, j, :],
                                                                 rhs=Sb[:, (cA + 1) * 128:(cA + 2) * 128], start=False, stop=True),
                     reads=[w["QB_"], SB_], writes=[PSO] if j == 3 else [])
            sq, ss, y1, sgg = w["sq"], w["ss"], w["y1"], w["sgg"]
            for j in range(4):
                P.op("act", lambda e, j=j, sq=sq, ss=ss: e.activation(out=sq[:, :], in_=psO[:, j * 128:(j + 1) * 128], func=AF.Square,
                                                                      accum_out=ss[:, j:j + 1]), reads=[PSO], writes=[w["SQ_"], w["SS_"]])
            P.op("act", lambda e, ss=ss: e.activation(out=ss[:, 4:8], in_=ss[:, 0:4], func=AF.Sqrt, bias=lbc[:, 3:4], scale=1.0 / 128.0),
                 reads=[w["SS_"], CONST], writes=[w["SS_"]])
            P.op("dve", lambda e, ss=ss: e.reciprocal(out=ss[:, 8:12], in_=ss[:, 4:8]), reads=[w["SS_"]], writes=[w["SS_"]])
            for j in range(4):
                P.op("dve", lambda e, j=j, ss=ss, y1=y1: e.scalar_tensor_tensor(out=y1[:, j, :], in0=psO[:, j * 128:(j + 1) * 128],
                                                                              scalar=ss[:, 8 + j:9 + j], in1=ngB[:, :], op0=ALU.mult, op1=ALU.mult),
                     reads=[PSO, w["SS_"], CONST], writes=[w["Y1_"]])
            P.op("act", lambda e, g=g, sgg=sgg: e.activation(out=sgg[:, :, :], in_=g[:, :, :], func=AF.Sigmoid), reads=[w["G_"]], writes=[w["SGG_"]])
            P.op("pool", lambda e, y1=y1, sgg=sgg: e.tensor_tensor(out=y1[:, :, :], in0=y1[:, :, :], in1=sgg[:, :, :], op=ALU.mult),
                 reads=[w["Y1_"], w["SGG_"]], writes=[w["Y1_"]])
            P.dma("sp", lambda e, y1=y1, c0=c0: e.dma_start(out=d["y_rec"][c0:c0 + 512, :].rearrange("(t p) v -> p t v", p=128), in_=y1[:, :, :]),
                  reads=[w["Y1_"]], is_out=True)
        P.finish()
        P.emit()
        es.close()


def get_M():
    if "M" not in _CACHE:
        _CACHE["M"] = MBuild().nc
    return _CACHE["M"]


POOL_WINDOWS = (2, 4, 8, 16)
_DBG = {}


def _run(nc, in_maps):
    res = run_bass_kernel_spmd(nc, in_maps, core_ids=list(range(8)))
    return res.results


def _mixer(l, zT, zt, rec_lb, rec_norm_g, conv_w, pool_w, pool_scale, consts):
    f32 = np.float32
    in_maps = []
    ZT = [np.concatenate([zT[4 * b + s] for s in range(4)], axis=1) for b in range(2)]
    Zt = [np.concatenate([zt[4 * b + s] for s in range(4)], axis=0) for b in range(2)]
    wch = np.repeat(np.array(POOL_WINDOWS, dtype=np.int64), 64)
    for c in range(8):
        b, j = c // 4, c % 4
        zc = np.zeros((3, 256, T + 2), f32)
        zp = np.zeros((256, T + 15), f32)
        lo = j * T
        for i in range(3):
            src = ZT[b][i * 256:(i + 1) * 256]
            if j == 0:
                zc[i, :, 2:] = src[:, 0:T]
            else:
                zc[i] = src[:, lo - 2:lo + T]
        srcp = ZT[b][1792:2048]
        if j == 0:
            zp[:, 15:] = srcp[:, 0:T]
        else:
            zp[:] = srcp[:, lo - 15:lo + T]
        tg = lo + np.arange(16)
        cnt = np.minimum(tg[None, :] + 1, wch[:, None])
        in_maps.append({
            "zq": np.ascontiguousarray(ZT[b][768 + j * 128:768 + (j + 1) * 128]),
            "zf": np.ascontiguousarray(ZT[b][1280 + j * 128:1280 + (j + 1) * 128]),
            "zv": np.ascontiguousarray(Zt[b][:, j * 128:(j + 1) * 128]),
            "zg": np.ascontiguousarray(Zt[b][:, 512 + j * 128:512 + (j + 1) * 128]),
            "lb2": np.ascontiguousarray(rec_lb[:, j * 128:(j + 1) * 128].T),
            "lsel": np.full((128, 1), float(l), f32),
            "ng": np.ascontiguousarray(rec_norm_g[l, j * 128:(j + 1) * 128][None, :]),
            "zc": zc, "cw": np.ascontiguousarray(conv_w[l].T), "zp": zp,
            "invc": (1.0 / cnt).astype(f32), "invw": np.ascontiguousarray((1.0 / wch).astype(f32).reshape(2, 128).T),
            "pw": np.ascontiguousarray(pool_w[l]), "pscale": np.ascontiguousarray(pool_scale[l].reshape(2, 128).T),
            "ident": consts["ident"], "cmask": consts["cmask"], "amask": consts["amask"],
        })
    res = _run(get_M(), in_maps)
    mixT = [np.zeros((D, SEQ), f32) for _ in range(2)]
    for c in range(8):
        b, j = c // 4, c % 4
        mixT[b][256 + j * 128:256 + (j + 1) * 128, :] = res[c]["y_rec"].T
        mixT[b][0:256, j * T:(j + 1) * T] = res[c]["y_cp"][0:256]
        mixT[b][768:1024, j * T:(j + 1) * T] = res[c]["y_cp"][256:512]
    return mixT


def kernel(x, mem, ffn1_gate, ffn1_up, ffn1_down, w_in, conv_w, rec_lb, rec_norm_g, pool_w, pool_scale, w_out,
           ca_q, ca_k, ca_v, ca_o, ffn2_gate, ffn2_up, ffn2_down, ln_g, ln_b):
    f32 = np.float32
    A = lambda a: np.ascontiguousarray(np.asarray(a, dtype=f32))
    x, mem = A(x), A(mem)
    ffn1_gate, ffn1_up, ffn1_down, w_in = A(ffn1_gate), A(ffn1_up), A(ffn1_down), A(w_in)
    conv_w, rec_lb, rec_norm_g, pool_w, pool_scale = A(conv_w), A(rec_lb), A(rec_norm_g), A(pool_w), A(pool_scale)
    w_out, ca_q, ca_k, ca_v, ca_o = A(w_out), A(ca_q), A(ca_k), A(ca_v), A(ca_o)
    ffn2_gate, ffn2_up, ffn2_down, ln_g, ln_b = A(ffn2_gate), A(ffn2_up), A(ffn2_down), A(ln_g), A(ln_b)
    ident = np.eye(128, dtype=f32)
    cmask = np.ones((128, 512), f32)
    cmask[:, ::64] = 0.0
    si = np.arange(128)
    amask = ((si[:, None] // 64 == si[None, :] // 64) & (si[:, None] <= si[None, :])).astype(f32)
    consts = {"ident": ident, "cmask": cmask, "amask": amask}

    def tile_cols(W, c0, ncol):
        K = W.shape[0]
        return W[:, c0:c0 + ncol].reshape(K // 128, 128, ncol).transpose(1, 0, 2).reshape(128, -1)

    def halves(W):
        return A(np.stack([tile_cols(W, hf * 512, 512) for hf in range(2)]))

    def ffn_layout(wg, wu, wd):
        gu = np.zeros((6, 128, 8192), f32)
        dn = np.zeros((6, 128, 4096), f32)
        for gi in range(6):
            c0 = gi * 512
            ncol = min(512, DFF - c0)
            gu[gi, :, 0:8 * ncol] = tile_cols(wg, c0, ncol)
            gu[gi, :, 4096:4096 + 8 * ncol] = tile_cols(wu, c0, ncol)
            nf = ncol // 128
            dn[gi, :, 0:nf * 1024] = wd[c0:c0 + ncol].reshape(nf, 128, D).transpose(1, 0, 2).reshape(128, -1)
        return gu, dn

    _lay = {}

    def part_b_inputs(l):
        if ("b", l) not in _lay:
            gu, dn = ffn_layout(ffn1_gate[l], ffn1_up[l], ffn1_down[l])
            _lay[("b", l)] = {"ffn1_gu": gu, "ffn1_dn": dn, "lnB_g": A(ln_g[l, 0:1]), "lnB_b": A(ln_b[l, 0:1]),
                              "w_in": A(np.stack([tile_cols(w_in[l], cg * 512, 512) for cg in range(6)]))}
        return _lay[("b", l)]

    def part_a_inputs(l, c, mixT):
        b, s = c // 4, c % 4
        if ("a", l) not in _lay:
            gu, dn = ffn_layout(ffn2_gate[l], ffn2_up[l], ffn2_down[l])
            _lay[("a", l)] = {"w_out": halves(w_out[l]), "ca_q": halves(ca_q[l]), "ca_k": halves(ca_k[l]),
                              "ca_v": halves(ca_v[l]), "ca_o": halves(ca_o[l]), "ffn2_gu": gu, "ffn2_dn": dn,
                              "lnA_g": A(ln_g[l, 1:4]), "lnA_b": A(ln_b[l, 1:4])}
        m = {"mixT": A(mixT[b][:, s * T:(s + 1) * T]), "mem": mem[b]}
        m.update(_lay[("a", l)])
        return m

    in_maps = []
    for c in range(8):
        b, s = c // 4, c % 4
        m = {"h_in": A(x[b, s * T:(s + 1) * T]), "ident": ident}
        m.update(part_b_inputs(0))
        in_maps.append(m)
    res = _run(get_P(False, True), in_maps)
    h = [r["h_out"] for r in res]
    for l in range(DEPTH):
        mixT = _mixer(l, [r["zT"] for r in res], [r["zt"] for r in res], rec_lb, rec_norm_g, conv_w, pool_w, pool_scale, consts)
        if _DBG is not None and "on" in _DBG:
            _DBG["mixT%d" % l] = mixT
            _DBG["h1_%d" % l] = h
        last = (l == DEPTH - 1)
        in_maps = []
        for c in range(8):
            m = {"h_in": h[c], "ident": ident}
            m.update(part_a_inputs(l, c, mixT))
            if not last:
                m.update(part_b_inputs(l + 1))
            in_maps.append(m)
        res = _run(get_P(True, not last), in_maps)
        h = [r["h_out"] for r in res]
        if _DBG is not None and "on" in _DBG:
            _DBG["hout_%d" % l] = h
    out = np.zeros((2, SEQ, D), f32)
    for c in range(8):
        b, s = c // 4, c % 4
        out[b, s * T:(s + 1) * T] = h[c]
    return out
```
